# Optimizing a Trainium2 kernel written in Bass

```python
import jax, jax.numpy as jnp
from jax import lax
import numpy as np

D_MODEL = 2048
BATCH = 2
SEQ = 8192
DEPTH = 4

HEAD_DIM = 128
SG_WIDTH = D_MODEL // 4
SG_HEADS = SG_WIDTH // HEAD_DIM
SG_CHUNK = 128
POOL_WINDOWS = (2, 4, 8, 16)
POOL_GROUPS = len(POOL_WINDOWS)
POOL_WIDTH = D_MODEL // 4
POOL_CH = POOL_WIDTH // POOL_GROUPS
NA_WIDTH = D_MODEL // 2
NA_HEADS = NA_WIDTH // HEAD_DIM
NA_KH = 8
NA_KW = 16
GRID_W = 64
MIX_WIDTH = SG_WIDTH + POOL_WIDTH + NA_WIDTH
IN_COLS = 2 * SG_WIDTH + POOL_WIDTH + 3 * NA_WIDTH
D_FF = 11 * D_MODEL // 4
EPS = 1e-6
NEG = -1e30

kernel_name = "hybrid_gmlp_pool_natten_macaron_encoder"


def rms_norm(x, g):
    xf = x.astype(jnp.float32)
    y = xf * lax.rsqrt(jnp.mean(xf * xf, axis=-1, keepdims=True) + EPS)
    return (y * g.astype(jnp.float32)).astype(x.dtype)


def swiglu(h, w_gate, w_up, w_down):
    return (jax.nn.silu(h @ w_gate) * (h @ w_up)) @ w_down


def spatial_gating(zu, zv, g, w_s, b_s):
    B, S, _ = zu.shape
    u = jax.nn.gelu(zu, approximate=False)
    v = jax.nn.gelu(zv, approximate=False).reshape(B, S // SG_CHUNK, SG_CHUNK, SG_HEADS, HEAD_DIM)
    v = rms_norm(v, g.reshape(SG_HEADS, HEAD_DIM))
    mixed = jnp.einsum('hpq,bnqhd->bnphd', w_s, v) + b_s.T[None, None, :, :, None]
    return u * mixed.reshape(B, S, SG_WIDTH)


def multiscale_pool(p, w, scale):
    B, S, _ = p.shape
    pf = p.astype(jnp.float32).reshape(B, S, POOL_GROUPS, POOL_CH)
    cs = jnp.concatenate([jnp.zeros((B, 1, POOL_GROUPS, POOL_CH), jnp.float32),
                          jnp.cumsum(pf, axis=1)], axis=1)
    t = jnp.arange(S)
    outs = []
    for g, win in enumerate(POOL_WINDOWS):
        lo = jnp.clip(t - win // 2, 0, S)
        hi = jnp.clip(t + win // 2, 0, S)
        cnt = (hi - lo).astype(jnp.float32)
        mean = (cs[:, hi, g] - cs[:, lo, g]) / cnt[None, :, None]
        outs.append(mean - pf[:, :, g])
    d = jnp.stack(outs, axis=2).astype(p.dtype)
    y = jnp.einsum('bsgc,gcd->bsgd', d, w) * scale.reshape(POOL_GROUPS, POOL_CH)
    return y.reshape(B, S, POOL_WIDTH)


def neighbourhood_attention(q, k, v, rpb):
    B, S, H, Dh = q.shape
    rows = S // GRID_W
    kh = min(NA_KH, rows)
    qg = q.reshape(B, rows, GRID_W, H, Dh)
    kg = k.reshape(B, rows, GRID_W, H, Dh)
    vg = v.reshape(B, rows, GRID_W, H, Dh)
    col = jnp.arange(GRID_W)
    col_start = jnp.clip(col - NA_KW // 2, 0, GRID_W - NA_KW)
    col_in = (col[None, :] >= col_start[:, None]) & (col[None, :] < col_start[:, None] + NA_KW)
    dc_idx = jnp.clip(col[None, :] - col[:, None] + NA_KW - 1, 0, 2 * NA_KW - 2)
    rpb_col = rpb.astype(jnp.float32)[:, :, dc_idx]
    scale = Dh ** -0.5

    def one_row(r):
        sr = jnp.clip(r - kh // 2, 0, rows - kh)
        q_r = lax.dynamic_index_in_dim(qg, r, axis=1, keepdims=False)
        k_r = lax.dynamic_slice_in_dim(kg, sr, kh, axis=1)
        v_r = lax.dynamic_slice_in_dim(vg, sr, kh, axis=1)
        dr = sr + jnp.arange(kh) - r + NA_KH - 1
        bias = jnp.take(rpb_col, dr, axis=1).transpose(0, 2, 1, 3)
        s = jnp.einsum('bqhd,bjkhd->bhqjk', q_r, k_r).astype(jnp.float32) * scale + bias[None]
        s = jnp.where(col_in[:, None, :], s, NEG)
        pr = jax.nn.softmax(s.reshape(B, H, GRID_W, kh * GRID_W), axis=-1)
        pr = pr.reshape(B, H, GRID_W, kh, GRID_W).astype(v.dtype)
        return jnp.einsum('bhqjk,bjkhd->bqhd', pr, v_r)

    out = lax.map(one_row, jnp.arange(rows))
    return out.transpose(1, 0, 2, 3, 4).reshape(B, S, H, Dh)


def setup_inputs(seed: int = 0) -> dict:
    key = jax.random.key(seed)
    ks = jax.random.split(key, 20)
    f32 = jnp.float32
    nrm = lambda k, shape, s: jax.random.normal(k, shape, f32) * s
    L = DEPTH
    return {
        "x": jax.random.normal(ks[0], (BATCH, SEQ, D_MODEL), f32),
        "ffn1_norm": 1.0 + nrm(ks[1], (L, D_MODEL), 0.05),
        "ffn1_w_gate": nrm(ks[2], (L, D_MODEL, D_FF), D_MODEL ** -0.5),
        "ffn1_w_up": nrm(ks[3], (L, D_MODEL, D_FF), D_MODEL ** -0.5),
        "ffn1_w_down": nrm(ks[4], (L, D_FF, D_MODEL), D_FF ** -0.5),
        "mix_norm": 1.0 + nrm(ks[5], (L, D_MODEL), 0.05),
        "w_in": nrm(ks[6], (L, D_MODEL, IN_COLS), D_MODEL ** -0.5),
        "sg_norm": 1.0 + nrm(ks[7], (L, SG_WIDTH), 0.05),
        "sg_w": nrm(ks[8], (L, SG_HEADS, SG_CHUNK, SG_CHUNK), SG_CHUNK ** -0.5),
        "sg_b": 1.0 + nrm(ks[9], (L, SG_HEADS, SG_CHUNK), 0.05),
        "pool_w": nrm(ks[10], (L, POOL_GROUPS, POOL_CH, POOL_CH), POOL_CH ** -0.5),
        "pool_scale": 1.0 + nrm(ks[11], (L, POOL_WIDTH), 0.1),
        "na_rpb": nrm(ks[12], (L, NA_HEADS, 2 * NA_KH - 1, 2 * NA_KW - 1), 0.1),
        "w_out": nrm(ks[13], (L, MIX_WIDTH, D_MODEL), MIX_WIDTH ** -0.5),
        "ffn2_norm": 1.0 + nrm(ks[14], (L, D_MODEL), 0.05),
        "ffn2_w_gate": nrm(ks[15], (L, D_MODEL, D_FF), D_MODEL ** -0.5),
        "ffn2_w_up": nrm(ks[16], (L, D_MODEL, D_FF), D_MODEL ** -0.5),
        "ffn2_w_down": nrm(ks[17], (L, D_FF, D_MODEL), D_FF ** -0.5),
        "final_norm": 1.0 + nrm(ks[18], (D_MODEL,), 0.05),
    }


def reference(x, ffn1_norm, ffn1_w_gate, ffn1_w_up, ffn1_w_down, mix_norm, w_in,
              sg_norm, sg_w, sg_b, pool_w, pool_scale, na_rpb, w_out,
              ffn2_norm, ffn2_w_gate, ffn2_w_up, ffn2_w_down, final_norm):
    B, S, _ = x.shape
    splits = [SG_WIDTH, 2 * SG_WIDTH, 2 * SG_WIDTH + POOL_WIDTH,
              2 * SG_WIDTH + POOL_WIDTH + NA_WIDTH, 2 * SG_WIDTH + POOL_WIDTH + 2 * NA_WIDTH]
    for l in range(DEPTH):
        x = x + 0.5 * swiglu(rms_norm(x, ffn1_norm[l]), ffn1_w_gate[l], ffn1_w_up[l], ffn1_w_down[l])
        h = rms_norm(x, mix_norm[l])
        z = h @ w_in[l]
        zu, zv, zp, zq, zk, zvv = jnp.split(z, splits, axis=-1)
        a = spatial_gating(zu, zv, sg_norm[l], sg_w[l], sg_b[l])
        bp = multiscale_pool(zp, pool_w[l], pool_scale[l])
        c = neighbourhood_attention(zq.reshape(B, S, NA_HEADS, HEAD_DIM),
                                    zk.reshape(B, S, NA_HEADS, HEAD_DIM),
                                    zvv.reshape(B, S, NA_HEADS, HEAD_DIM),
                                    na_rpb[l]).reshape(B, S, NA_WIDTH)
        x = x + jnp.concatenate([a, bp, c], axis=-1) @ w_out[l]
        x = x + 0.5 * swiglu(rms_norm(x, ffn2_norm[l]), ffn2_w_gate[l], ffn2_w_up[l], ffn2_w_down[l])
    return rms_norm(x, final_norm)
```

```python
import numpy as np
import ml_dtypes
import concourse.bass as bass
import concourse.mybir as mybir
from concourse.bass_utils import run_bass_kernel_spmd
from contextlib import ExitStack

F32 = mybir.dt.float32
BF16 = mybir.dt.bfloat16
I32 = mybir.dt.int32
U8 = mybir.dt.uint8
AF = mybir.ActivationFunctionType
ALU = mybir.AluOpType

NCORES = 8
L = 4
D = 2048
DFF = 5632
TOK = 2048
TT = 1024
NT = TOK // TT
EPS = 1e-6
NEG = -1e30
ATT_SCALE = 128 ** -0.5
NVEC = 68

PAGE = 256
DT_SIZE = {F32: 4, BF16: 2, I32: 4, U8: 1}


class SbView:
    def __init__(self, ap, lo, hi, shape, dtype):
        self.ap = ap
        self.lo = lo
        self.hi = hi
        self.shape = shape
        self.dtype = dtype

    def pages(self, e0=None, e1=None):
        if e0 is None:
            lo, hi = self.lo, self.hi
        else:
            sz = DT_SIZE[self.dtype]
            lo, hi = self.lo + e0 * sz, self.lo + e1 * sz
        return [("sb", p) for p in range(lo // PAGE, (hi - 1) // PAGE + 1)]


class Sched:
    ENGS = ("pe", "act", "dve", "pool", "sp")

    def __init__(self, nc, es, arena_bytes):
        self.nc = nc
        self.es = es
        self.arena_bytes = arena_bytes
        self.arena = es.enter_context(nc.sbuf_tensor("arena", [128, arena_bytes], U8))
        self.top = 0
        self.dry = False
        self.reset()
        self.sem = {e: es.enter_context(nc.semaphore("s_" + e)) for e in self.ENGS}
        self.dsems = {}

    def reset(self):
        self.streams = {e: [] for e in self.ENGS}
        self.cnt = {e: 0 for e in self.ENGS}
        self.seen = {e: {} for e in self.ENGS}
        self.res = {}
        self.n_ops = 0
        if hasattr(self, "dsems"):
            for d in self.dsems.values():
                d[1] = 0

    def alloc(self, nbytes, align=PAGE):
        lo = (self.top + align - 1) // align * align
        self.top = lo + nbytes
        assert self.top <= self.arena_bytes, f"SBUF arena overflow {self.top} > {self.arena_bytes}"
        return lo

    def view(self, lo, shape, dtype):
        parts = shape[0]
        n = int(np.prod(shape[1:]))
        nb = n * DT_SIZE[dtype]
        assert lo + nb <= self.arena_bytes
        ap = self.arena[0:parts, lo:lo + nb].bitcast(dtype)
        if len(shape) == 3:
            ap = ap.rearrange("p (a b) -> p a b", b=shape[2])
        elif len(shape) == 4:
            ap = ap.rearrange("p (a b c) -> p a b c", b=shape[2], c=shape[3])
        return SbView(ap, lo, lo + nb, shape, dtype)

    def tile(self, shape, dtype, align=PAGE):
        n = int(np.prod(shape[1:])) * DT_SIZE[dtype]
        lo = self.alloc(n, align)
        return self.view(lo, shape, dtype)

    def psum(self, name):
        return self.es.enter_context(self.nc.psum_tensor(name, [128, 512], F32))

    def dsem(self, key):
        if key not in self.dsems:
            self.dsems[key] = [self.es.enter_context(self.nc.semaphore("d_" + key)), 0]
        return self.dsems[key]

    def _deps(self, reads, writes):
        toks = []
        res = self.res
        for r in reads:
            e = res.get(r)
            if e is not None and e[0] is not None:
                toks.append(e[0])
        for w in writes:
            e = res.get(w)
            if e is not None:
                if e[0] is not None:
                    toks.append(e[0])
                toks.extend(e[1])
        return toks

    def _commit(self, reads, writes, tok):
        res = self.res
        for r in reads:
            e = res.get(r)
            if e is None:
                res[r] = [None, [tok]]
            else:
                if not e[1] or e[1][-1] != tok:
                    e[1].append(tok)
        for w in writes:
            res[w] = [tok, []]

    def _waits(self, eng, toks, skip_same=False):
        best = {}
        for (sk, v) in toks:
            if skip_same and sk == eng:
                continue
            if v > best.get(sk, 0):
                best[sk] = v
        out = []
        seen = self.seen[eng]
        for sk, v in best.items():
            if seen.get(sk, 0) >= v:
                continue
            seen[sk] = v
            out.append((sk, v))
        return out

    def _semh(self, sk):
        if sk in self.sem:
            return self.sem[sk]
        return self.dsems[sk][0]

    def op(self, eng, fn, reads=(), writes=(), skip_same=False):
        if self.dry:
            return None
        toks = self._deps(reads, writes)
        waits = self._waits(eng, toks, skip_same)
        self.cnt[eng] += 1
        tok = (eng, self.cnt[eng])
        self.streams[eng].append((waits, fn, (eng, 1)))
        self._commit(reads, writes, tok)
        self.n_ops += 1
        return tok

    def dma(self, eng, key, pairs, reads=(), writes=()):
        if self.dry:
            return None
        d = self.dsem(key)
        toks = self._deps(reads, writes)
        if d[1] > 0:
            toks.append((key, d[1]))
        waits = self._waits(eng, toks)
        for i, (o, a) in enumerate(pairs):
            d[1] += 16
            self.streams[eng].append((waits if i == 0 else [], ("dma", o, a), (key, 16)))
        tok = (key, d[1])
        self._commit(reads, writes, tok)
        self.n_ops += 1
        return tok

    def wait_all(self, eng):
        toks = []
        for e in self.ENGS:
            if self.cnt[e] > 0:
                toks.append((e, self.cnt[e]))
        for k, d in self.dsems.items():
            if d[1] > 0:
                toks.append((k, d[1]))
        waits = self._waits(eng, toks)
        self.streams[eng].append((waits, None, None))

    def emit(self):
        nc = self.nc
        block = self.es.enter_context(nc.Block())
        handles = {"pe": block.tensor, "act": block.scalar, "dve": block.vector,
                   "pool": block.gpsimd, "sp": block.sync}
        for e in self.ENGS:
            stream = self.streams[e]
            if not stream:
                continue

            def body(h, stream=stream):
                for waits, fn, inc in stream:
                    for sk, v in waits:
                        h.wait_ge(self._semh(sk), v)
                    if fn is None:
                        continue
                    if isinstance(fn, tuple):
                        ins = h.dma_start(out=fn[1], in_=fn[2])
                    else:
                        ins = fn(h)
                    if inc is not None:
                        ins.then_inc(self._semh(inc[0]), inc[1])
            handles[e](body)


class WStream:
    def __init__(self, S, nslots, slot_elems):
        self.S = S
        self.n = nslots
        self.slot_elems = slot_elems
        self.slots = [S.tile([128, slot_elems], BF16) for _ in range(nslots)]
        self.specs = []
        self.pos = 0
        self.loaded = 0

    def start_real(self):
        self.pos = 0
        self.loaded = 0

    def next(self, dram_ap, nelem):
        if self.S.dry:
            self.specs.append((dram_ap, nelem))
            return self.slots[(len(self.specs) - 1) % self.n]
        i = self.pos
        self.pos += 1
        while self.loaded < min(len(self.specs), i + self.n):
            c = self.loaded
            ap, ne = self.specs[c]
            slot = self.slots[c % self.n]
            self.S.dma("pool", f"w{c % self.n}", [(slot.ap[:, 0:ne], ap)], writes=slot.pages(0, ne))
            self.loaded += 1
        return self.slots[i % self.n]


class Prog:
    def __init__(self, phases, fused):
        self.phases = phases
        self.fused = fused
        self.nc = nc = bass.Bass("TRN2", target_bir_lowering=False)
        self.in_names = []
        self.out_names = []
        first_kind, first_l = phases[0]
        last_kind, last_l = phases[-1]
        self.layers = sorted(set(l for _, l in phases))

        def dt_in(name, shape, dt):
            self.in_names.append(name)
            return nc.dram_tensor(name, shape, dt, kind="ExternalInput").ap()

        def dt_out(name, shape, dt):
            self.out_names.append(name)
            return nc.dram_tensor(name, shape, dt, kind="ExternalOutput").ap()

        def dt_int(name, shape, dt):
            return nc.dram_tensor(name, shape, dt).ap()

        self.dt_in, self.dt_out, self.dt_int = dt_in, dt_out, dt_int
        W = {}
        for l in self.layers:
            kinds = [k for k, ll in phases if ll == l]
            w = {}
            if "A" in kinds:
                w["f1_gu"] = dt_in(f"f1_gu{l}", [44, 128, 2 * 16 * 128], F32)
                w["f1_d"] = dt_in(f"f1_d{l}", [2, 16, 128, 22 * 128], F32)
                w["win_f"] = dt_in(f"win_f{l}", [12, 128, 2 * 16 * 128], F32)
                w["win_t"] = dt_in(f"win_t{l}", [3, 128, 16 * 512], F32)
                w["sgn"] = dt_in(f"sgn{l}", [1, 512], F32)
            if "B" in kinds:
                w["f2_gu"] = dt_in(f"f2_gu{l}", [44, 128, 2 * 16 * 128], F32)
                w["f2_d"] = dt_in(f"f2_d{l}", [2, 16, 128, 22 * 128], F32)
                w["wout"] = dt_in(f"wout{l}", [8, 128, 2 * 16 * 128], F32)
                w["sgwT"] = dt_in(f"sgwT{l}", [128, 4 * 128], F32)
                w["sgb"] = dt_in(f"sgb{l}", [1, 512], F32)
                w["poolw"] = dt_in(f"poolw{l}", [128, 4 * 128], F32)
                w["bias0"] = dt_in(f"bias0_{l}", [8, 128, 16 * 128], F32)
                w["bias1"] = dt_in(f"bias1_{l}", [8, 128, 16 * 128], F32)
            W[l] = w
        self.W = W
        self.vecs_d = dt_in("vecs", [128, L * NVEC], F32)
        self.invc_d = dt_in("invc", [4, TOK], F32)
        zshapes = {"zu": ([4, 128, TOK], BF16), "zp": ([4, 128, TOK], F32), "zq": ([8, 128, TOK], BF16),
                   "zk": ([8, 128, TOK], BF16), "zvn": ([16, 128, 512], BF16), "zvv": ([16, 128, 1024], BF16)}
        hshapes = {"khp": ([8, 128, 256], BF16), "khn": ([8, 128, 192], BF16), "vhp": ([2, 128, 1024], BF16),
                   "vhn": ([2, 128, 1024], BF16), "php": ([4, 128, 8], F32), "phn": ([4, 128, 8], F32)}
        self.zshapes, self.hshapes = zshapes, hshapes
        if not fused:
            if first_kind == "A":
                self.x_src = dt_in("x_in", [16, 128, TOK], F32)
            else:
                self.x_src = dt_in("x_in", [16, 128, TOK], F32)
                self.z_src = {k: dt_in(k + "_in", s, d) for k, (s, d) in zshapes.items()}
                self.halo = {k: dt_in(k, s, d) for k, (s, d) in hshapes.items()}
            self.x_dst = dt_out("x_out", [16, 128, TOK], F32)
            if last_kind == "A":
                self.z_dst = {k: dt_out(k + "_out", s, d) for k, (s, d) in zshapes.items()}
        else:
            raise NotImplementedError

        self.es = ExitStack()
        with self.es:
            self.S = S = Sched(nc, self.es, 212000)
            self.setup()
            S.dry = True
            self.program()
            S.dry = False
            S.reset()
            self.ws.start_real()
            self.program()
            S.wait_all("sp")
            self.n_ops = S.n_ops
            S.emit()

    def setup(self):
        S = self.S
        self.ps = [S.psum(f"ps{i}") for i in range(8)]
        self.ones = S.tile([128, 128], BF16)
        self.onesn = S.tile([128, 128], BF16)
        self.onesrow = S.tile([1, 128], BF16)
        self.epsc = S.tile([128, 1], F32)
        self.vecs = S.tile([128, L * NVEC], F32)
        self.sq = [S.tile([128, 512], BF16) for _ in range(4)]
        self.rstd = S.tile([128, 2, 512], F32)
        self.sil = [S.tile([128, 512], F32) for _ in range(2)]
        self.ws = WStream(S, 4, 4096)
        self.XT = S.tile([128, 16, TT], F32)
        self.H = S.tile([128, 16, TT], BF16)
        self.HID = S.tile([128, 22, TT], BF16)
        self.MIX = self.H
        hb = self.HID.lo
        o = hb
        self.WT = []
        for i in range(2):
            self.WT.append(S.view(o, [128, 16, 512], BF16)); o += 16384
        self.stg_bf = []
        for i in range(2):
            self.stg_bf.append(S.view(o, [128, TT], BF16)); o += 2048
        self.stg_f = []
        for i in range(2):
            self.stg_f.append(S.view(o, [128, TT], F32)); o += 4096
        assert o <= self.HID.hi
        self.V32 = S.view(self.XT.lo + 49152, [128, 8, 512], F32)
        self.SS = S.tile([128, 32], F32)
        self.SGN = S.tile([128, 512], F32)
        self.vstg = [S.tile([128, 512], BF16) for _ in range(2)]
        self.junk = S.tile([128, 128], F32)
        o = self.XT.lo
        self.UT = S.view(o, [128, 4, TT], BF16); o += 8192
        self.VN = S.view(o, [128, 8, 512], BF16); o += 8192
        self.SGW = S.view(o, [128, 4, 128], BF16); o += 1024
        self.SGB = S.view(o, [1, 512], BF16); o += 1024
        self.POOLW = S.view(o, [128, 4, 128], BF16); o += 1024
        self.PB = []
        self.PA = []
        self.PC = []
        self.INVC = []
        self.DD = []
        for i in range(2):
            self.PB.append(S.view(o, [128, 1040], F32)); o += 4352
            self.PA.append(S.view(o, [128, 1040], F32)); o += 4352
            self.PC.append(S.view(o, [128, 1040], F32)); o += 4352
            self.INVC.append(S.view(o, [128, TT], F32)); o += 4096
            self.DD.append(S.view(o, [128, TT], BF16)); o += 2048
        assert o <= self.XT.hi, (o, self.XT.hi)
        o = hb
        self.QT, self.KT, self.VV, self.BIAS = [], [], [], []
        for i in range(2):
            self.QT.append(S.view(o, [128, TT], BF16)); o += 2048
            self.KT.append(S.view(o, [128, 1472], BF16)); o += 3072
            self.VV.append(S.view(o, [128, 12, 128], BF16)); o += 3072
            self.BIAS.append(S.view(o, [128, 16, 128], F32)); o += 8192
        self.T = []
        self.PT = []
        for i in range(2):
            self.T.append(S.view(o, [128, 6, 128], F32)); o += 3072
            self.PT.append(S.view(o, [128, 6, 128], BF16)); o += 1536
        self.RECIP = S.view(o, [128, 512], F32); o += 2048
        assert o <= self.HID.hi, (o, self.HID.hi)

    def program(self):
        S = self.S
        S.op("dve", lambda e: e.memset(self.ones.ap, 1.0), writes=self.ones.pages())
        S.op("dve", lambda e: e.memset(self.onesn.ap, 1.0 / D), writes=self.onesn.pages())
        S.op("dve", lambda e: e.memset(self.onesrow.ap, 1.0), writes=self.onesrow.pages())
        S.op("dve", lambda e: e.memset(self.epsc.ap, EPS), writes=self.epsc.pages())
        S.dma("sp", "cst", [(self.vecs.ap, self.vecs_d)], writes=self.vecs.pages())
        ph = self.phases
        i = 0
        while i < len(ph):
            grp = [ph[i]]
            if ph[i][0] == "B" and i + 1 < len(ph) and ph[i + 1][0] == "A":
                grp.append(ph[i + 1])
            i += len(grp)
            for kind, l in grp:
                if kind == "A":
                    S.dma("sp", "cst2", [(self.SGN.ap, self.W[l]["sgn"].partition_broadcast(128))],
                          writes=self.SGN.pages())
            for t in range(NT):
                for gi, (kind, l) in enumerate(grp):
                    if kind == "B":
                        self.emit_B(l, t)
                        if len(grp) == 1:
                            if l == 3:
                                self.emit_norm(0, 52, final=True)
                            self.store_x(t, self.x_dst)
                    else:
                        self.emit_A(l, t, load_x=(len(grp) == 1))

    def vcol(self, l, off, c):
        return self.vecs.ap[:, l * NVEC + off + c: l * NVEC + off + c + 1]

    def load_x(self, t):
        S = self.S
        for c0 in range(0, 16, 4):
            S.dma("sp", f"xl{c0 // 4}",
                  [(self.XT.ap[:, c0:c0 + 4, :], self.x_src[c0:c0 + 4, :, t * TT:(t + 1) * TT].rearrange("c p t -> p c t"))],
                  reads=[("x", t)], writes=self.XT.pages(c0 * TT, (c0 + 4) * TT))

    def store_x(self, t, dst):
        S = self.S
        for c0 in range(0, 16, 4):
            S.dma("sp", f"xs{c0 // 4}",
                  [(dst[c0:c0 + 4, :, t * TT:(t + 1) * TT].rearrange("c p t -> p c t"), self.XT.ap[:, c0:c0 + 4, :])],
                  reads=self.XT.pages(c0 * TT, (c0 + 4) * TT), writes=[("x", t)])

    def emit_norm(self, l, voff, final=False):
        S = self.S
        XT, H = self.XT, self.H
        for half in range(2):
            bank = 6 + half
            cs = slice(half * 512, (half + 1) * 512)
            for c in range(16):
                sq = self.sq[c % 4]
                S.op("act", lambda e, sq=sq, c=c, cs=cs: e.activation(out=sq.ap, in_=XT.ap[:, c, cs], func=AF.Square),
                     reads=XT.pages(c * TT + half * 512, c * TT + half * 512 + 512), writes=sq.pages())
                S.op("pe", lambda e, sq=sq, c=c, bank=bank: e.matmul(self.ps[bank][:, :], self.onesn.ap, sq.ap,
                                                                     start=(c == 0), stop=(c == 15)),
                     reads=sq.pages() + self.onesn.pages(), writes=[("ps", bank)], skip_same=True)
        for half in range(2):
            bank = 6 + half
            S.op("act", lambda e, half=half, bank=bank: e.activation(out=self.rstd.ap[:, half, :], in_=self.ps[bank][:, :],
                                                                     func=AF.Sqrt, bias=self.epsc.ap[:, 0:1], scale=1.0),
                 reads=[("ps", bank)] + self.epsc.pages(), writes=self.rstd.pages(half * 512, half * 512 + 512))
            S.op("dve", lambda e, half=half: e.reciprocal(out=self.rstd.ap[:, half, :], in_=self.rstd.ap[:, half, :]),
                 reads=self.rstd.pages(half * 512, half * 512 + 512), writes=self.rstd.pages(half * 512, half * 512 + 512))
        for half in range(2):
            cs = slice(half * 512, (half + 1) * 512)
            for c in range(16):
                e0 = c * TT + half * 512
                if final:
                    S.op("dve", lambda e, c=c, cs=cs, half=half: e.scalar_tensor_tensor(
                        out=XT.ap[:, c, cs], in0=XT.ap[:, c, cs], scalar=self.vcol(l, voff, c),
                        in1=self.rstd.ap[:, half, :], op0=ALU.mult, op1=ALU.mult),
                         reads=XT.pages(e0, e0 + 512) + self.rstd.pages(half * 512, half * 512 + 512) + self.vecs.pages(),
                         writes=XT.pages(e0, e0 + 512))
                else:
                    S.op("dve", lambda e, c=c, cs=cs, half=half: e.scalar_tensor_tensor(
                        out=H.ap[:, c, cs], in0=XT.ap[:, c, cs], scalar=self.vcol(l, voff, c),
                        in1=self.rstd.ap[:, half, :], op0=ALU.mult, op1=ALU.mult),
                         reads=XT.pages(e0, e0 + 512) + self.rstd.pages(half * 512, half * 512 + 512) + self.vecs.pages(),
                         writes=H.pages(e0, e0 + 512))

    def emit_ffn(self, l, which):
        S = self.S
        XT, H, HID = self.XT, self.H, self.HID
        w = self.W[l]
        gu = w["f1_gu" if which == 1 else "f2_gu"]
        dn = w["f1_d" if which == 1 else "f2_d"]
        self.emit_norm(l, 0 if which == 1 else 32)
        ps = self.ps
        dcnt = 0
        for hh in range(2):
            for jj in range(22):
                j = hh * 22 + jj
                wv = self.ws.next(gu[j], 4096)
                wap = wv.ap.rearrange("p (s k m) -> p s k m", s=2, k=16)
                for half in range(2):
                    cs = slice(half * 512, (half + 1) * 512)
                    for s in range(2):
                        bank = 2 * half + s

                        def mm(e, wap=wap, s=s, cs=cs, bank=bank):
                            for k in range(16):
                                ins = e.matmul(ps[bank][:, :], wap[:, s, k, :], H.ap[:, k, cs], start=(k == 0), stop=(k == 15))
                            return ins
                        hreads = []
                        for k in range(16):
                            hreads += H.pages(k * TT + half * 512, k * TT + half * 512 + 512)
                        S.op("pe", mm, reads=wv.pages(s * 2048, s * 2048 + 2048) + hreads, writes=[("ps", bank)], skip_same=True)
                    sil = self.sil[half]
                    S.op("act", lambda e, sil=sil, half=half: e.activation(out=sil.ap, in_=ps[2 * half][:, :], func=AF.Silu),
                         reads=[("ps", 2 * half)], writes=sil.pages())
                    e0 = jj * TT + half * 512
                    S.op("dve", lambda e, sil=sil, half=half, jj=jj, cs=cs: e.tensor_tensor(
                        out=HID.ap[:, jj, cs], in0=sil.ap, in1=ps[2 * half + 1][:, :], op=ALU.mult),
                         reads=sil.pages() + [("ps", 2 * half + 1)], writes=HID.pages(e0, e0 + 512))
            for i in range(16):
                wv = self.ws.next(dn[hh, i], 22 * 128)
                wap = wv.ap[:, 0:22 * 128].rearrange("p (j m) -> p j m", j=22)
                for half in range(2):
                    cs = slice(half * 512, (half + 1) * 512)
                    bank = 4 + (dcnt % 2)
                    dcnt += 1

                    def mm(e, wap=wap, cs=cs, bank=bank):
                        for jj in range(22):
                            ins = e.matmul(ps[bank][:, :], wap[:, jj, :], HID.ap[:, jj, cs], start=(jj == 0), stop=(jj == 21))
                        return ins
                    hreads = []
                    for jj in range(22):
                        hreads += HID.pages(jj * TT + half * 512, jj * TT + half * 512 + 512)
                    S.op("pe", mm, reads=wv.pages(0, 22 * 128) + hreads, writes=[("ps", bank)], skip_same=True)
                    e0 = i * TT + half * 512
                    S.op("dve", lambda e, i=i, cs=cs, bank=bank: e.scalar_tensor_tensor(
                        out=XT.ap[:, i, cs], in0=ps[bank][:, :], scalar=0.5, in1=XT.ap[:, i, cs],
                        op0=ALU.mult, op1=ALU.add),
                         reads=[("ps", bank)] + XT.pages(e0, e0 + 512), writes=XT.pages(e0, e0 + 512))

    def emit_A(self, l, t, load_x):
        S = self.S
        if load_x:
            self.load_x(t)
        self.emit_ffn(l, 1)
        self.emit_proj(l, t)

    def emit_proj(self, l, t):
        S = self.S
        XT, H = self.XT, self.H
        w = self.W[l]
        ps = self.ps
        zd = self.z_dst
        self.emit_norm(l, 16)
        self.store_x(t, self.x_dst)
        tsl = slice(t * TT, (t + 1) * TT)
        def load_wt(g):
            wt = self.WT[g % 2]
            S.dma("pool", f"wt{g % 2}", [(wt.ap.rearrange("p k c -> p (k c)"), w["win_t"][g])], writes=wt.pages())
        load_wt(0)
        load_wt(1)
        nb = 0
        nbf = 0
        nf = 0
        for pj in range(12):
            wv = self.ws.next(w["win_f"][pj], 4096)
            wap = wv.ap.rearrange("p (s k m) -> p s k m", s=2, k=16)
            for s in range(2):
                c = 2 * pj + s
                if 4 <= c < 8:
                    stg = self.stg_f[nf % 2]; nf += 1
                else:
                    stg = self.stg_bf[nbf % 2]; nbf += 1
                for half in range(2):
                    cs = slice(half * 512, (half + 1) * 512)
                    bank = nb % 4
                    nb += 1

                    def mm(e, wap=wap, s=s, cs=cs, bank=bank):
                        for k in range(16):
                            ins = e.matmul(ps[bank][:, :], wap[:, s, k, :], H.ap[:, k, cs], start=(k == 0), stop=(k == 15))
                        return ins
                    hreads = []
                    for k in range(16):
                        hreads += H.pages(k * TT + half * 512, k * TT + half * 512 + 512)
                    S.op("pe", mm, reads=wv.pages(s * 2048, s * 2048 + 2048) + hreads, writes=[("ps", bank)], skip_same=True)
                    wr = stg.pages(half * 512, half * 512 + 512)
                    if c < 4:
                        S.op("act", lambda e, stg=stg, cs=cs, bank=bank: e.activation(out=stg.ap[:, cs], in_=ps[bank][:, :], func=AF.Gelu),
                             reads=[("ps", bank)], writes=wr)
                    elif c % 2 == 0:
                        S.op("act", lambda e, stg=stg, cs=cs, bank=bank: e.activation(out=stg.ap[:, cs], in_=ps[bank][:, :], func=AF.Copy),
                             reads=[("ps", bank)], writes=wr)
                    else:
                        S.op("dve", lambda e, stg=stg, cs=cs, bank=bank: e.tensor_copy(out=stg.ap[:, cs], in_=ps[bank][:, :]),
                             reads=[("ps", bank)], writes=wr)
                if c < 4:
                    dst = zd["zu"][c, :, tsl]
                elif c < 8:
                    dst = zd["zp"][c - 4, :, tsl]
                elif c < 16:
                    dst = zd["zq"][c - 8, :, tsl]
                else:
                    dst = zd["zk"][c - 16, :, tsl]
                S.dma("sp", f"zs{c % 4}", [(dst, stg.ap)], reads=stg.pages(), writes=[("z", l % 2, c, t)])
        nv = 0
        for g in range(3):
            wt = self.WT[g % 2]
            for blk in range(8):
                bank = 4 + (nb % 4)
                nb += 1
                bs = slice(blk * 128, (blk + 1) * 128)

                def mm(e, wt=wt, bs=bs, bank=bank):
                    for k in range(16):
                        ins = e.matmul(ps[bank][:, :], H.ap[:, k, bs], wt.ap[:, k, :], start=(k == 0), stop=(k == 15))
                    return ins
                hreads = []
                for k in range(16):
                    hreads += H.pages(k * TT + blk * 128, k * TT + blk * 128 + 128)
                S.op("pe", mm, reads=wt.pages() + hreads, writes=[("ps", bank)], skip_same=True)
                gb = t * 8 + blk
                if g == 0:
                    S.op("act", lambda e, blk=blk, bank=bank: e.activation(out=self.V32.ap[:, blk, :], in_=ps[bank][:, :], func=AF.Gelu),
                         reads=[("ps", bank)], writes=self.V32.pages(blk * 512, blk * 512 + 512))
                    for h in range(4):
                        S.op("act", lambda e, blk=blk, h=h: e.activation(
                            out=self.junk.ap, in_=self.V32.ap[:, blk, h * 128:(h + 1) * 128], func=AF.Square,
                            accum_out=self.SS.ap[:, blk * 4 + h: blk * 4 + h + 1]),
                             reads=self.V32.pages(blk * 512 + h * 128, blk * 512 + h * 128 + 128),
                             writes=self.junk.pages() + self.SS.pages())
                else:
                    vs = self.vstg[nv % 2]
                    nv += 1
                    if blk % 2 == 0:
                        S.op("act", lambda e, vs=vs, bank=bank: e.activation(out=vs.ap, in_=ps[bank][:, :], func=AF.Copy),
                             reads=[("ps", bank)], writes=vs.pages())
                    else:
                        S.op("dve", lambda e, vs=vs, bank=bank: e.tensor_copy(out=vs.ap, in_=ps[bank][:, :]),
                             reads=[("ps", bank)], writes=vs.pages())
                    S.dma("sp", f"vs{nv % 2}", [(zd["zvv"][gb, :, (g - 1) * 512:g * 512], vs.ap)],
                          reads=vs.pages(), writes=[("z", l % 2, "vv", gb, g)])
            if g == 0:
                load_wt(2)
                S.op("act", lambda e: e.activation(out=self.SS.ap, in_=self.SS.ap, func=AF.Sqrt,
                                                   bias=self.epsc.ap[:, 0:1], scale=1.0 / 128),
                     reads=self.SS.pages() + self.epsc.pages(), writes=self.SS.pages())
                S.op("dve", lambda e: e.reciprocal(out=self.SS.ap, in_=self.SS.ap), reads=self.SS.pages(), writes=self.SS.pages())
                for blk in range(8):
                    vs = self.vstg[nv % 2]
                    nv += 1

                    def nrm(e, blk=blk, vs=vs):
                        for h in range(4):
                            hs = slice(h * 128, (h + 1) * 128)
                            ins = e.scalar_tensor_tensor(out=vs.ap[:, hs], in0=self.V32.ap[:, blk, hs],
                                                         scalar=self.SS.ap[:, blk * 4 + h: blk * 4 + h + 1],
                                                         in1=self.SGN.ap[:, hs], op0=ALU.mult, op1=ALU.mult)
                        return ins
                    S.op("dve", nrm, reads=self.V32.pages(blk * 512, blk * 512 + 512) + self.SS.pages() + self.SGN.pages(),
                         writes=vs.pages())
                    S.dma("sp", f"vs{nv % 2}", [(zd["zvn"][t * 8 + blk], vs.ap)], reads=vs.pages(),
                          writes=[("z", l % 2, "vn", t * 8 + blk)])

    def emit_B(self, l, t):
        S = self.S
        w = self.W[l]
        ps = self.ps
        zs, hl = self.z_src, self.halo
        par = l % 2
        XT, MIX = self.XT, self.MIX
        tsl = slice(t * TT, (t + 1) * TT)
        hk = lambda n: [("halo", par, n)]
        S.dma("sp", "bg0", [(self.UT.ap, zs["zu"][:, :, tsl].rearrange("c p t -> p c t"))],
              reads=[("z", par, c, t) for c in range(4)], writes=self.UT.pages())
        S.dma("sp", "bg1", [(self.VN.ap, zs["zvn"][t * 8:(t + 1) * 8].rearrange("b p c -> p b c"))],
              reads=[("z", par, "vn", t * 8 + b) for b in range(8)], writes=self.VN.pages())
        S.dma("pool", "bg2", [(self.SGW.ap.rearrange("p h c -> p (h c)"), w["sgwT"]),
                              (self.SGB.ap, w["sgb"]),
                              (self.POOLW.ap.rearrange("p h c -> p (h c)"), w["poolw"])],
              writes=self.SGW.pages() + self.SGB.pages() + self.POOLW.pages())
        n = 0
        for h in range(4):
            hs = slice(h * 128, (h + 1) * 128)
            for g4 in range(2):
                bank = n % 2
                n += 1

                def mm(e, h=h, hs=hs, g4=g4, bank=bank):
                    for q in range(4):
                        nn = g4 * 4 + q
                        o = ps[bank][:, q * 128:(q + 1) * 128]
                        e.matmul(o, self.VN.ap[:, nn, hs], self.SGW.ap[:, h, :], start=True, stop=False)
                        ins = e.matmul(o, self.onesrow.ap[0:1, :], self.SGB.ap[0:1, hs], start=False, stop=True)
                    return ins
                S.op("pe", mm, reads=self.VN.pages() + self.SGW.pages() + self.SGB.pages() + self.onesrow.pages(),
                     writes=[("ps", bank)], skip_same=True)
                e0 = h * TT + g4 * 512
                S.op("dve", lambda e, h=h, g4=g4, bank=bank: e.tensor_tensor(
                    out=MIX.ap[:, h, g4 * 512:(g4 + 1) * 512], in0=ps[bank][:, :],
                    in1=self.UT.ap[:, h, g4 * 512:(g4 + 1) * 512], op=ALU.mult),
                     reads=[("ps", bank)] + self.UT.pages(e0, e0 + 512), writes=MIX.pages(e0, e0 + 512))
        for g in range(4):
            ix = g % 2
            P, A, C, IV, DDv = self.PB[ix], self.PA[ix], self.PC[ix], self.INVC[ix], self.DD[ix]
            if t == 0:
                S.dma("sp", f"pl{ix}", [(P.ap[:, 0:8], hl["php"][g]), (P.ap[:, 8:1040], zs["zp"][g][:, 0:1032])],
                      reads=hk("php") + [("z", par, 4 + g, 0), ("z", par, 4 + g, 1)], writes=P.pages())
            else:
                S.dma("sp", f"pl{ix}", [(P.ap[:, 0:1032], zs["zp"][g][:, 1016:2048]), (P.ap[:, 1032:1040], hl["phn"][g])],
                      reads=hk("phn") + [("z", par, 4 + g, 0), ("z", par, 4 + g, 1)], writes=P.pages())
            S.dma("sp", f"pi{ix}", [(IV.ap, self.invc_d[g:g + 1, tsl].partition_broadcast(128))], writes=IV.pages())

            def pool(e, g=g, P=P, A=A, C=C, IV=IV, DDv=DDv):
                e.tensor_tensor(out=A.ap[:, 1:1040], in0=P.ap[:, 0:1039], in1=P.ap[:, 1:1040], op=ALU.add)
                Sx, Ox = A, C
                if g >= 1:
                    e.tensor_tensor(out=C.ap[:, 2:1039], in0=A.ap[:, 1:1038], in1=A.ap[:, 3:1040], op=ALU.add)
                    Sx, Ox = C, A
                if g >= 2:
                    e.tensor_tensor(out=A.ap[:, 4:1037], in0=C.ap[:, 2:1035], in1=C.ap[:, 6:1039], op=ALU.add)
                    Sx, Ox = A, C
                if g >= 3:
                    e.tensor_tensor(out=C.ap[:, 8:1033], in0=A.ap[:, 4:1029], in1=A.ap[:, 12:1037], op=ALU.add)
                    Sx, Ox = C, A
                e.tensor_tensor(out=Ox.ap[:, 8:1032], in0=Sx.ap[:, 8:1032], in1=IV.ap, op=ALU.mult)
                return e.tensor_tensor(out=DDv.ap, in0=Ox.ap[:, 8:1032], in1=P.ap[:, 8:1032], op=ALU.subtract)
            S.op("dve", pool, reads=P.pages() + IV.pages(), writes=A.pages() + C.pages() + DDv.pages())
            for half in range(2):
                bank = 2 + half
                cs = slice(half * 512, (half + 1) * 512)
                S.op("pe", lambda e, g=g, DDv=DDv, cs=cs, bank=bank: e.matmul(ps[bank][:, :], self.POOLW.ap[:, g, :], DDv.ap[:, cs],
                                                                            start=True, stop=True),
                     reads=self.POOLW.pages() + DDv.pages(half * 512, half * 512 + 512), writes=[("ps", bank)], skip_same=True)
                e0 = (4 + g) * TT + half * 512
                S.op("act", lambda e, g=g, cs=cs, bank=bank: e.activation(out=MIX.ap[:, 4 + g, cs], in_=ps[bank][:, :], func=AF.Copy,
                                                                         scale=self.vcol(l, 48, g)),
                     reads=[("ps", bank)] + self.vecs.pages(), writes=MIX.pages(e0, e0 + 512))
        nblk = 0
        for h in range(8):
            ix = h % 2
            QT, KT, VV, BI = self.QT[ix], self.KT[ix], self.VV[ix], self.BIAS[ix]
            hs = slice(h * 128, (h + 1) * 128)
            gv = 1 + h // 4
            S.dma("sp", f"aq{ix}", [(QT.ap, zs["zq"][h][:, tsl])], reads=[("z", par, 8 + h, t)], writes=QT.pages())
            if t == 0:
                S.dma("sp", f"ak{ix}", [(KT.ap[:, 0:256], hl["khp"][h]), (KT.ap[:, 256:1472], zs["zk"][h][:, 0:1216])],
                      reads=hk("khp") + [("z", par, 16 + h, 0), ("z", par, 16 + h, 1)], writes=KT.pages())
                S.dma("sp", f"av{ix}", [(VV.ap[:, 0:2, :], hl["vhp"][0:2, :, hs].rearrange("b p c -> p b c")),
                                        (VV.ap[:, 2:12, :], zs["zvv"][0:10, :, hs].rearrange("b p c -> p b c"))],
                      reads=hk("vhp") + [("z", par, "vv", gb, gv) for gb in range(10)], writes=VV.pages())
            else:
                S.dma("sp", f"ak{ix}", [(KT.ap[:, 0:1280], zs["zk"][h][:, 768:2048]), (KT.ap[:, 1280:1472], hl["khn"][h])],
                      reads=hk("khn") + [("z", par, 16 + h, 0), ("z", par, 16 + h, 1)], writes=KT.pages())
                S.dma("sp", f"av{ix}", [(VV.ap[:, 0:10, :], zs["zvv"][6:16, :, hs].rearrange("b p c -> p b c")),
                                        (VV.ap[:, 10:12, :], hl["vhn"][0:2, :, hs].rearrange("b p c -> p b c"))],
                      reads=hk("vhn") + [("z", par, "vv", gb, gv) for gb in range(6, 16)], writes=VV.pages())
            S.dma("sp", f"ab{ix}", [(BI.ap.rearrange("p a b -> p (a b)"), w["bias0" if t == 0 else "bias1"][h])], writes=BI.pages())
            for b in range(8):
                if t == 0 and b == 0:
                    ext0, nfull, hf, boff = 0, 6, False, 0
                elif t == 0 and b == 1:
                    ext0, nfull, hf, boff = 1, 5, False, 6
                elif t == 0:
                    ext0, nfull, hf, boff = b, 4, True, 11
                elif b < 6:
                    ext0, nfull, hf, boff = b, 4, True, 0
                elif b == 6:
                    ext0, nfull, hf, boff = 6, 4, True, 5
                else:
                    ext0, nfull, hf, boff = 6, 5, True, 10
                nch = nfull + (1 if hf else 0)
                ti = nblk % 2
                bA, bB = 2 * ti, 2 * ti + 1
                nblk += 1
                T, PT = self.T[ti], self.PT[ti]
                qs = slice(b * 128, (b + 1) * 128)

                def mmS(e, KT=KT, QT=QT, ext0=ext0, nfull=nfull, nch=nch, qs=qs, bA=bA, bB=bB):
                    for c in range(nch):
                        M = 128 if c < nfull else 64
                        ec = ext0 + c
                        o = ps[bA][0:M, c * 128:(c + 1) * 128] if c < 4 else ps[bB][0:M, (c - 4) * 128:(c - 3) * 128]
                        ins = e.matmul(o, KT.ap[:, ec * 128:ec * 128 + M], QT.ap[:, qs], start=True, stop=True)
                    return ins
                S.op("pe", mmS, reads=KT.pages() + QT.pages(b * 128, b * 128 + 128), writes=[("ps", bA), ("ps", bB)], skip_same=True)

                def evS(e, T=T, BI=BI, nfull=nfull, hf=hf, boff=boff, bA=bA, bB=bB):
                    ins = e.scalar_tensor_tensor(out=T.ap[:, 0:4, :], in0=ps[bA][:, :].rearrange("p (a b) -> p a b", b=128),
                                                 scalar=ATT_SCALE, in1=BI.ap[:, boff:boff + 4, :], op0=ALU.mult, op1=ALU.add)
                    if nfull > 4:
                        nx = nfull - 4
                        ins = e.scalar_tensor_tensor(out=T.ap[:, 4:nfull, :],
                                                     in0=ps[bB][:, 0:nx * 128].rearrange("p (a b) -> p a b", b=128),
                                                     scalar=ATT_SCALE, in1=BI.ap[:, boff + 4:boff + nfull, :],
                                                     op0=ALU.mult, op1=ALU.add)
                    if hf:
                        c0 = (nfull - 4) * 128
                        ins = e.scalar_tensor_tensor(out=T.ap[0:64, nfull, :], in0=ps[bB][0:64, c0:c0 + 128],
                                                     scalar=ATT_SCALE, in1=BI.ap[0:64, boff + nfull, :],
                                                     op0=ALU.mult, op1=ALU.add)
                    return ins
                S.op("dve", evS, reads=[("ps", bA), ("ps", bB)] + BI.pages(), writes=T.pages())

                def exS(e, T=T, PT=PT, nfull=nfull, hf=hf):
                    ins = e.activation(out=PT.ap[:, 0:nfull, :], in_=T.ap[:, 0:nfull, :], func=AF.Exp)
                    if hf:
                        ins = e.activation(out=PT.ap[0:64, nfull, :], in_=T.ap[0:64, nfull, :], func=AF.Exp)
                    return ins
                S.op("act", exS, reads=T.pages(), writes=PT.pages())
                bq = b % 4
                oB = 4 + 2 * ((nblk - 1) // 4 % 2)
                dB = oB + 1
                os_ = slice(bq * 128, (bq + 1) * 128)

                def mmO(e, VV=VV, PT=PT, ext0=ext0, nfull=nfull, nch=nch, oB=oB, dB=dB, os_=os_):
                    for c in range(nch):
                        M = 128 if c < nfull else 64
                        e.matmul(ps[oB][:, os_], VV.ap[0:M, ext0 + c, :], PT.ap[0:M, c, :], start=(c == 0), stop=(c == nch - 1))
                    for c in range(nch):
                        M = 128 if c < nfull else 64
                        ins = e.matmul(ps[dB][:, os_], self.ones.ap[0:M, :], PT.ap[0:M, c, :], start=(c == 0), stop=(c == nch - 1))
                    return ins
                S.op("pe", mmO, reads=VV.pages() + PT.pages() + self.ones.pages(), writes=[("ps", oB), ("ps", dB)], skip_same=True)
                if bq == 3:
                    g4 = b // 4
                    e0 = (8 + h) * TT + g4 * 512

                    def fin(e, h=h, g4=g4, oB=oB, dB=dB):
                        e.reciprocal(out=self.RECIP.ap, in_=ps[dB][:, :])
                        return e.tensor_tensor(out=MIX.ap[:, 8 + h, g4 * 512:(g4 + 1) * 512], in0=ps[oB][:, :],
                                               in1=self.RECIP.ap, op=ALU.mult)
                    S.op("dve", fin, reads=[("ps", oB), ("ps", dB)], writes=self.RECIP.pages() + MIX.pages(e0, e0 + 512))
        self.load_x(t)
        n = 0
        for pj in range(8):
            wv = self.ws.next(w["wout"][pj], 4096)
            wap = wv.ap.rearrange("p (s k m) -> p s k m", s=2, k=16)
            for s in range(2):
                i = 2 * pj + s
                for half in range(2):
                    cs = slice(half * 512, (half + 1) * 512)
                    bank = n % 4
                    n += 1

                    def mm(e, wap=wap, s=s, cs=cs, bank=bank):
                        for k in range(16):
                            ins = e.matmul(ps[bank][:, :], wap[:, s, k, :], MIX.ap[:, k, cs], start=(k == 0), stop=(k == 15))
                        return ins
                    mreads = []
                    for k in range(16):
                        mreads += MIX.pages(k * TT + half * 512, k * TT + half * 512 + 512)
                    S.op("pe", mm, reads=wv.pages(s * 2048, s * 2048 + 2048) + mreads, writes=[("ps", bank)], skip_same=True)
                    e0 = i * TT + half * 512
                    S.op("dve", lambda e, i=i, cs=cs, bank=bank: e.tensor_tensor(out=XT.ap[:, i, cs], in0=ps[bank][:, :],
                                                                                in1=XT.ap[:, i, cs], op=ALU.add),
                         reads=[("ps", bank)] + XT.pages(e0, e0 + 512), writes=XT.pages(e0, e0 + 512))
        self.emit_ffn(l, 2)

    def run(self, in_maps):
        res = run_bass_kernel_spmd(self.nc, in_maps, core_ids=list(range(NCORES)))
        return res.results


def c_(a):
    return np.ascontiguousarray(a, dtype=np.float32)


def prep_gu(wg, wu):
    a = wg.reshape(16, 128, 44, 128).transpose(2, 1, 0, 3)
    b = wu.reshape(16, 128, 44, 128).transpose(2, 1, 0, 3)
    return c_(np.stack([a, b], axis=2)).reshape(44, 128, 2 * 16 * 128)


def prep_d(wd):
    return c_(wd.reshape(2, 22, 128, 16, 128).transpose(0, 3, 2, 1, 4)).reshape(2, 16, 128, 22 * 128)


def prep_pairs(w, npair):
    return c_(w.reshape(16, 128, npair, 2, 128).transpose(2, 1, 3, 0, 4)).reshape(npair, 128, 2 * 16 * 128)


def prep_layer_weights(inp, l):
    w = {}
    w[f"f1_gu{l}"] = prep_gu(inp["ffn1_w_gate"][l], inp["ffn1_w_up"][l])
    w[f"f1_d{l}"] = prep_d(inp["ffn1_w_down"][l])
    w[f"f2_gu{l}"] = prep_gu(inp["ffn2_w_gate"][l], inp["ffn2_w_up"][l])
    w[f"f2_d{l}"] = prep_d(inp["ffn2_w_down"][l])
    win = inp["w_in"][l]
    wf = np.concatenate([win[:, 0:512], win[:, 1024:1536], win[:, 1536:2560], win[:, 2560:3584]], axis=1)
    w[f"win_f{l}"] = prep_pairs(wf, 12)
    wt = np.concatenate([win[:, 512:1024], win[:, 3584:4608]], axis=1)
    w[f"win_t{l}"] = c_(wt.reshape(16, 128, 3, 512).transpose(2, 1, 0, 3)).reshape(3, 128, 16 * 512)
    w[f"wout{l}"] = prep_pairs(inp["w_out"][l], 8)
    w[f"sgwT{l}"] = c_(inp["sg_w"][l].transpose(2, 0, 1)).reshape(128, 512)
    w[f"sgb{l}"] = c_(inp["sg_b"][l]).reshape(1, 512)
    w[f"sgn{l}"] = c_(inp["sg_norm"][l]).reshape(1, 512)
    w[f"poolw{l}"] = c_(inp["pool_w"][l].transpose(1, 0, 2)).reshape(128, 512)
    return w


def prep_vecs(inp):
    v = np.zeros((128, L, NVEC), np.float32)
    for l in range(L):
        v[:, l, 0:16] = inp["ffn1_norm"][l].reshape(16, 128).T
        v[:, l, 16:32] = inp["mix_norm"][l].reshape(16, 128).T
        v[:, l, 32:48] = inp["ffn2_norm"][l].reshape(16, 128).T
        v[:, l, 48:52] = inp["pool_scale"][l].reshape(4, 128).T
        v[:, l, 52:68] = inp["final_norm"].reshape(16, 128).T
    return v.reshape(128, L * NVEC)


def prep_invc(core):
    pos = core % 4
    tg = pos * TOK + np.arange(TOK)
    out = np.zeros((4, TOK), np.float32)
    for g, win in enumerate((2, 4, 8, 16)):
        lo = np.clip(tg - win // 2, 0, 8192)
        hi = np.clip(tg + win // 2, 0, 8192)
        out[g] = 1.0 / (hi - lo).astype(np.float32)
    return out


def bias_block(rpb_l, qrows, krows):
    qr = np.repeat(np.asarray(qrows), 64)
    qc = np.tile(np.arange(64), len(qrows))
    kr = np.repeat(np.asarray(krows), 64)
    kc = np.tile(np.arange(64), len(krows))
    sr = np.clip(qr - 4, 0, 120)
    cs = np.clip(qc - 8, 0, 48)
    ok = (kr[:, None] >= sr[None, :]) & (kr[:, None] < sr[None, :] + 8) & \
         (kc[:, None] >= cs[None, :]) & (kc[:, None] < cs[None, :] + 16) & \
         (kr[:, None] >= 0) & (kr[:, None] < 128)
    dr = np.clip(kr[:, None] - qr[None, :] + 7, 0, 14)
    dc = np.clip(kc[:, None] - qc[None, :] + 15, 0, 30)
    vals = rpb_l[:, dr, dc]
    return np.where(ok[None], vals, np.float32(NEG)).astype(np.float32)


def prep_bias(rpb_l, core):
    pos = core % 4
    r0 = pos * 32

    def chunks(qrows, k0, nrows, nch):
        b = bias_block(rpb_l, qrows, list(range(k0, k0 + nrows)))
        K = b.shape[1]
        pad = np.full((8, nch * 128, 128), NEG, np.float32)
        pad[:, :K] = b
        return pad.reshape(8, nch, 128, 128).transpose(0, 2, 1, 3)
    t0 = chunks((r0, r0 + 1), r0 - 4, 12, 6)
    t1 = chunks((r0 + 2, r0 + 3), r0 - 2, 10, 5)
    g = chunks((r0 + 8, r0 + 9), r0 + 4, 9, 5)
    b14 = chunks((r0 + 28, r0 + 29), r0 + 24, 9, 5)
    b15 = chunks((r0 + 30, r0 + 31), r0 + 24, 11, 6)
    bias0 = np.ascontiguousarray(np.concatenate([t0, t1, g], axis=2)).reshape(8, 128, 16 * 128)
    bias1 = np.ascontiguousarray(np.concatenate([g, b14, b15], axis=2)).reshape(8, 128, 16 * 128)
    return bias0, bias1


_PROGS = {}


def get_prog(key, phases, fused=False):
    if key not in _PROGS:
        _PROGS[key] = Prog(phases, fused)
    return _PROGS[key]


def roll_vecs(vecs, slots):
    v = vecs.reshape(128, L, NVEC)
    out = v.copy()
    for virt, real in slots.items():
        out[:, virt] = v[:, real]
    return np.ascontiguousarray(out).reshape(128, L * NVEC)


def rename_layer(wd, real, virt):
    out = {}
    for k, v in wd.items():
        if k.startswith("bias0_") or k.startswith("bias1_"):
            out[k[:6] + str(virt)] = v
        else:
            assert k.endswith(str(real))
            out[k[:-1] + str(virt)] = v
    return out


def make_halos(zouts, core):
    pos = core % 4
    bf = ml_dtypes.bfloat16
    h = {}
    if pos > 0:
        p = zouts[core - 1]
        h["khp"] = np.ascontiguousarray(p["zk_out"][:, :, TOK - 256:TOK])
        h["vhp"] = np.ascontiguousarray(p["zvv_out"][14:16])
        h["php"] = np.ascontiguousarray(p["zp_out"][:, :, TOK - 8:TOK])
    else:
        h["khp"] = np.zeros((8, 128, 256), bf)
        h["vhp"] = np.zeros((2, 128, 1024), bf)
        h["php"] = np.zeros((4, 128, 8), np.float32)
    if pos < 3:
        n = zouts[core + 1]
        h["khn"] = np.ascontiguousarray(n["zk_out"][:, :, 0:192])
        h["vhn"] = np.ascontiguousarray(n["zvv_out"][0:2])
        h["phn"] = np.ascontiguousarray(n["zp_out"][:, :, 0:8])
    else:
        h["khn"] = np.zeros((8, 128, 192), bf)
        h["vhn"] = np.zeros((2, 128, 1024), bf)
        h["phn"] = np.zeros((4, 128, 8), np.float32)
    return h


def kernel(**inputs):
    inp = {k: np.asarray(v) for k, v in inputs.items()}
    x = inp["x"].astype(np.float32, copy=False)
    xs = x.reshape(NCORES, TOK, D)
    xT = [np.ascontiguousarray(xs[c].T).reshape(16, 128, TOK) for c in range(NCORES)]
    vecs = prep_vecs(inp)
    invc = [prep_invc(c) for c in range(NCORES)]
    LW = [prep_layer_weights(inp, l) for l in range(L)]
    BI = [[prep_bias(inp["na_rpb"][l].astype(np.float32), c) for c in range(NCORES)] for l in range(L)]

    def wsel(l, kind):
        if kind == "A":
            names = [f"f1_gu{l}", f"f1_d{l}", f"win_f{l}", f"win_t{l}", f"sgn{l}"]
        else:
            names = [f"f2_gu{l}", f"f2_d{l}", f"wout{l}", f"sgwT{l}", f"sgb{l}", f"poolw{l}"]
        return {n: LW[l][n] for n in names}

    pA = get_prog("A", [("A", 0)])
    maps = []
    for c in range(NCORES):
        m = {"x_in": xT[c], "vecs": vecs, "invc": invc[c]}
        m.update(wsel(0, "A"))
        maps.append(m)
    outs = pA.run(maps)
    pBA = get_prog("BA", [("B", 0), ("A", 1)])
    for l in range(L - 1):
        maps = []
        vr = roll_vecs(vecs, {0: l, 1: l + 1})
        for c in range(NCORES):
            m = {"x_in": outs[c]["x_out"], "vecs": vr, "invc": invc[c]}
            for k in ("zu", "zp", "zq", "zk", "zvn", "zvv"):
                m[k + "_in"] = outs[c][k + "_out"]
            m.update(make_halos(outs, c))
            m.update(rename_layer(wsel(l, "B"), l, 0))
            m["bias0_0"], m["bias1_0"] = BI[l][c]
            m.update(rename_layer(wsel(l + 1, "A"), l + 1, 1))
            maps.append(m)
        outs = pBA.run(maps)
    pBF = get_prog("BF", [("B", 3)])
    maps = []
    for c in range(NCORES):
        m = {"x_in": outs[c]["x_out"], "vecs": vecs, "invc": invc[c]}
        for k in ("zu", "zp", "zq", "zk", "zvn", "zvv"):
            m[k + "_in"] = outs[c][k + "_out"]
        m.update(make_halos(outs, c))
        m.update(wsel(3, "B"))
        m["bias0_3"], m["bias1_3"] = BI[3][c]
        maps.append(m)
    outs = pBF.run(maps)
    y = np.empty((NCORES, TOK, D), np.float32)
    for c in range(NCORES):
        y[c] = np.asarray(outs[c]["x_out"], dtype=np.float32).reshape(D, TOK).T
    return y.reshape(2, 8192, D)
```
